# Optimizing a Trainium2 kernel written in Bass

```python
import jax, jax.numpy as jnp
from jax import lax
import numpy as np

D_MODEL = 1024
BATCH = 4
SEQ = 4096
DEPTH = 4

CHUNK = 64
N_MIXERS = 2
HEAD_SIZE = 64
N_HEADS = D_MODEL // HEAD_SIZE
D_FF = 4 * D_MODEL
CONV_WIDTH = 3
D_DECAY_LORA = 64
D_AAA_LORA = 64
D_MV_LORA = 32
D_GATE_LORA = 160
N_CONV_LAYERS = (DEPTH + 1) // 2
N_RWKV_LAYERS = DEPTH // 2
N_VRES = max(N_RWKV_LAYERS - 1, 0)
N_MOD = 6
NORM_EPS = 1e-6
GN_EPS = 64e-5

kernel_name = "hybrid_shortconv_rwkv7_adaln_encoder"


def _rmsnorm(x, g):
    x32 = x.astype(jnp.float32)
    y = x32 * lax.rsqrt(jnp.mean(x32 * x32, axis=-1, keepdims=True) + NORM_EPS)
    return (y * g.astype(jnp.float32)).astype(x.dtype)


def _modulate(h, shift, scale):
    return h * (1 + scale[:, None, :]) + shift[:, None, :]


def _short_conv_mixer(h, w_in, conv_w, w_out):
    bch = h @ w_in
    b_gate, c_gate, hv = jnp.split(bch, 3, axis=-1)
    u = c_gate * hv
    conv = lax.conv_general_dilated(
        u, conv_w[:, None, :], window_strides=(1,),
        padding=[(CONV_WIDTH - 1, 0)],
        dimension_numbers=('NWC', 'WIO', 'NWC'),
        feature_group_count=D_MODEL)
    return (b_gate * conv) @ w_out


def _wkv7_scan(r, w, k, v, a, b):
    bsz, seq = r.shape[0], r.shape[1]

    def to_chunks(t):
        return t.transpose(1, 0, 2, 3).reshape(seq // CHUNK, CHUNK, bsz, N_HEADS, HEAD_SIZE)

    def frame_step(state, inp):
        r_t, w_t, k_t, v_t, a_t, b_t = inp
        sa = jnp.einsum('bhvk,bhk->bhv', state, a_t)
        state = (state * w_t[:, :, None, :]
                 + sa[..., None] * b_t[:, :, None, :]
                 + v_t[..., None] * k_t[:, :, None, :])
        y_t = jnp.einsum('bhvk,bhk->bhv', state, r_t)
        return state, y_t

    def chunk_step(state, inp_chunk):
        return lax.scan(frame_step, state, inp_chunk)

    s0 = jnp.zeros((bsz, N_HEADS, HEAD_SIZE, HEAD_SIZE), jnp.float32)
    inputs = (to_chunks(r), to_chunks(w), to_chunks(k), to_chunks(v), to_chunks(a), to_chunks(b))
    _, y = lax.scan(chunk_step, s0, inputs)
    return y.reshape(seq, bsz, N_HEADS, HEAD_SIZE).transpose(1, 0, 2, 3)


def _rwkv7_mixer(h, v_first, vres, mu, w_rkv, w_o, w0, w1, w2, a0, a1, a2,
                 g1, g2, k_k, k_a, r_k, ln_w, ln_b):
    bsz, seq, d = h.shape
    h_prev = jnp.pad(h, ((0, 0), (1, 0), (0, 0)))[:, :-1]
    xx = h_prev - h
    xr, xw, xk, xv, xa, xg = h[None] + xx[None] * mu[:, None, None, :]
    r, k, v = jnp.einsum('nbsd,nde->nbse', jnp.stack([xr, xk, xv]), w_rkv)
    w = -jax.nn.softplus(-(w0 + jnp.tanh(xw @ w1) @ w2)) - 0.5
    a = jax.nn.sigmoid(a0 + (xa @ a1) @ a2)
    g = jax.nn.sigmoid(xg @ g1) @ g2
    if vres is None:
        v_first = v
    else:
        v0, v1, v2 = vres
        v = v + (v_first - v) * jax.nn.sigmoid(v0 + (xv @ v1) @ v2)

    def heads(t):
        return t.reshape(bsz, seq, N_HEADS, HEAD_SIZE).astype(jnp.float32)

    kk = heads(k * k_k)
    kk = kk / jnp.maximum(jnp.sqrt(jnp.sum(kk * kk, axis=-1, keepdims=True)), 1e-12)
    k = k * (1 + (a - 1) * k_a)
    rh, kh, vh, ah = heads(r), heads(k), heads(v), heads(a)
    decay = jnp.exp(-jnp.exp(heads(w)))
    y = _wkv7_scan(rh, decay, kh, vh, -kk, kk * ah)
    mean = jnp.mean(y, axis=-1, keepdims=True)
    var = jnp.mean(jnp.square(y - mean), axis=-1, keepdims=True)
    y = (y - mean) * lax.rsqrt(var + GN_EPS)
    y = y * ln_w.reshape(N_HEADS, HEAD_SIZE) + ln_b.reshape(N_HEADS, HEAD_SIZE)
    bonus = jnp.sum(rh * kh * r_k, axis=-1, keepdims=True) * vh
    out = ((y + bonus).reshape(bsz, seq, d).astype(h.dtype) * g) @ w_o
    return out, v_first


def setup_inputs(seed: int = 0) -> dict:
    key = jax.random.key(seed)
    ks = iter(jax.random.split(key, 40))
    D = D_MODEL
    nrm = lambda shape, s: jax.random.normal(next(ks), shape, jnp.float32) * s
    uni = lambda shape, lo, hi: jax.random.uniform(next(ks), shape, jnp.float32, lo, hi)
    NC, NR, NV = N_CONV_LAYERS, N_RWKV_LAYERS, N_VRES
    return {
        "x": nrm((BATCH, SEQ, D), 1.0),
        "c": nrm((BATCH, D), 1.0),
        "norm_g": 1.0 + nrm((DEPTH, 2, D), 0.05),
        "final_g": 1.0 + nrm((D,), 0.05),
        "ada_w": nrm((DEPTH, D, N_MOD * D), 0.5 * D ** -0.5),
        "ada_b": nrm((DEPTH, N_MOD * D), 0.02),
        "conv_w_in": nrm((NC, D, 3 * D), D ** -0.5),
        "conv_w": nrm((NC, CONV_WIDTH, D), CONV_WIDTH ** -0.5),
        "conv_w_out": nrm((NC, D, D), D ** -0.5),
        "rw_mu": uni((NR, 6, D), 0.0, 1.0),
        "rw_w_rkv": nrm((NR, 3, D, D), D ** -0.5),
        "rw_w_o": nrm((NR, D, D), D ** -0.5),
        "rw_w0": uni((NR, D), -6.5, -1.5),
        "rw_w1": nrm((NR, D, D_DECAY_LORA), D ** -0.5),
        "rw_w2": nrm((NR, D_DECAY_LORA, D), 0.1 * D_DECAY_LORA ** -0.5),
        "rw_a0": nrm((NR, D), 0.1),
        "rw_a1": nrm((NR, D, D_AAA_LORA), D ** -0.5),
        "rw_a2": nrm((NR, D_AAA_LORA, D), 0.1 * D_AAA_LORA ** -0.5),
        "rw_g1": nrm((NR, D, D_GATE_LORA), D ** -0.5),
        "rw_g2": nrm((NR, D_GATE_LORA, D), D_GATE_LORA ** -0.5),
        "rw_k_k": 0.85 + nrm((NR, D), 0.05),
        "rw_k_a": 1.0 + nrm((NR, D), 0.05),
        "rw_r_k": nrm((NR, N_HEADS, HEAD_SIZE), 0.1),
        "rw_ln_w": 1.0 + nrm((NR, D), 0.05),
        "rw_ln_b": nrm((NR, D), 0.02),
        "rw_v0": 1.0 + nrm((NV, D), 0.1),
        "rw_v1": nrm((NV, D, D_MV_LORA), D ** -0.5),
        "rw_v2": nrm((NV, D_MV_LORA, D), 0.1 * D_MV_LORA ** -0.5),
        "mlp_w1": nrm((DEPTH, D, D_FF), D ** -0.5),
        "mlp_w2": nrm((DEPTH, D_FF, D), D_FF ** -0.5),
    }


def reference(x, c, norm_g, final_g, ada_w, ada_b, conv_w_in, conv_w, conv_w_out,
              rw_mu, rw_w_rkv, rw_w_o, rw_w0, rw_w1, rw_w2, rw_a0, rw_a1, rw_a2,
              rw_g1, rw_g2, rw_k_k, rw_k_a, rw_r_k, rw_ln_w, rw_ln_b,
              rw_v0, rw_v1, rw_v2, mlp_w1, mlp_w2):
    c_act = jax.nn.silu(c)
    v_first = None
    for i in range(DEPTH):
        mod = c_act @ ada_w[i] + ada_b[i]
        sh1, sc1, gt1, sh2, sc2, gt2 = jnp.split(mod, N_MOD, axis=-1)
        h = _modulate(_rmsnorm(x, norm_g[i, 0]), sh1, sc1)
        j = i // N_MIXERS
        if i % N_MIXERS == 0:
            y = _short_conv_mixer(h, conv_w_in[j], conv_w[j], conv_w_out[j])
        else:
            vres = None if v_first is None else (rw_v0[j - 1], rw_v1[j - 1], rw_v2[j - 1])
            y, v_first = _rwkv7_mixer(
                h, v_first, vres, rw_mu[j], rw_w_rkv[j], rw_w_o[j], rw_w0[j], rw_w1[j],
                rw_w2[j], rw_a0[j], rw_a1[j], rw_a2[j], rw_g1[j], rw_g2[j], rw_k_k[j],
                rw_k_a[j], rw_r_k[j], rw_ln_w[j], rw_ln_b[j])
        x = x + gt1[:, None, :] * y
        h = _modulate(_rmsnorm(x, norm_g[i, 1]), sh2, sc2)
        x = x + gt2[:, None, :] * (jnp.square(jax.nn.relu(h @ mlp_w1[i])) @ mlp_w2[i])
    return _rmsnorm(x, final_g)
```

```python
import contextlib
import numpy as np
import concourse.bass as bass
import concourse.mybir as mybir
from concourse.bass_utils import run_bass_kernel_spmd

F32 = mybir.dt.float32
BF16 = mybir.dt.bfloat16
AF = mybir.ActivationFunctionType
ALU = mybir.AluOpType

T = 2048
NT = 512
NR = 128
LC = 64
NCH = NR // LC
D = 1024
KC = 8
DEPTH = 4
EPS = 1e-6
GN_EPS = 64e-5
CW = 0.6065306597126334

V_NG = 0
V_FG = 64
V_ADAB = 72
V_CONVW = 264
V_MU = 312
V_W0 = 408
V_A0 = 424
V_KK = 440
V_KA = 456
V_RK = 472
V_LNW = 488
V_LNB = 504
V_V0 = 520
V_C = 528
V_HM = 536
NV = 537
C_ID = 0
C_M5 = 128
C_ID2 = 448
NCONST = 512


class Tl:
    __slots__ = ("name", "w", "r")

    def __init__(self, name):
        self.name = name
        self.w = None
        self.r = []


def alias_seed(new_tls, old_tls):
    toks = []
    for o in old_tls:
        if o.w is not None:
            toks.append(o.w)
        toks.extend(o.r)
    for n in new_tls:
        n.r.extend(toks)


class Sched:
    ENG = ("pe", "act", "dve", "pool", "sp")
    EPOCH = 30000

    def __init__(self, nc):
        self.nc = nc
        self.ops = {e: [] for e in self.ENG}
        self.ncomp = {e: 0 for e in self.ENG}
        self.chans = {}

    @staticmethod
    def _deps(reads, writes):
        deps = []
        for t in reads:
            if t.w is not None:
                deps.append(t.w)
        for t in writes:
            if t.w is not None:
                deps.append(t.w)
            deps.extend(t.r)
        return deps

    def _commit(self, tok, reads, writes):
        for t in reads:
            t.r.append(tok)
        for t in writes:
            t.w = tok
            t.r = []

    def op(self, eng, fn, reads=(), writes=()):
        deps = self._deps(reads, writes)
        n = self.ncomp[eng]
        self.ncomp[eng] = n + 1
        tok = ("e", (eng, n // self.EPOCH), n % self.EPOCH + 1)
        if eng == "pe":
            deps = [d for d in deps if not (d[0] == "e" and d[1][0] == "pe")]
        self.ops[eng].append((fn, deps, tok, 1))
        self._commit(tok, reads, writes)
        return tok

    def dma(self, eng, chan, fn, reads=(), writes=(), inc=16):
        deps = self._deps(reads, writes)
        self.chans[chan] = self.chans.get(chan, 0) + inc
        tok = ("d", chan, self.chans[chan])
        self.ops[eng].append((fn, deps, tok, inc))
        self._commit(tok, reads, writes)
        return tok

    def emit(self, final_waits=()):
        nc = self.nc
        with contextlib.ExitStack() as st:
            sems = {}
            for e in self.ENG:
                for ep in range(self.ncomp[e] // self.EPOCH + 1):
                    sems[("e", (e, ep))] = st.enter_context(nc.semaphore("s_%s%d" % (e, ep)))
            for c in self.chans:
                sems[("d", c)] = st.enter_context(nc.semaphore("c_" + str(c)))
            block = st.enter_context(nc.Block())

            def mk(ename):
                ops = self.ops[ename]

                def body(eng):
                    waited = {}
                    for fn, deps, tok, inc in ops:
                        need = {}
                        for d in deps:
                            k = (d[0], d[1])
                            if d[2] > need.get(k, 0):
                                need[k] = d[2]
                        for k, v in need.items():
                            if waited.get(k, 0) >= v:
                                continue
                            eng.wait_ge(sems[k], v)
                            waited[k] = v
                        ins = fn(eng)
                        ins.then_inc(sems[(tok[0], tok[1])], inc)
                    if ename == "sp":
                        for d in final_waits:
                            eng.wait_ge(sems[(d[0], d[1])], d[2])
                return body

            block.tensor(mk("pe"))
            block.scalar(mk("act"))
            block.vector(mk("dve"))
            block.gpsimd(mk("pool"))
            block.sync(mk("sp"))


WEIGHT_SHAPES = {
    "ada_w": [4, 1024, 6144], "conv_w_in": [2, 1024, 3072], "conv_w_out": [2, 1024, 1024],
    "rw_w_rkv": [2, 3, 1024, 1024], "rw_w_o": [2, 1024, 1024], "rw_w1": [2, 1024, 64], "rw_w2": [2, 64, 1024],
    "rw_a1": [2, 1024, 64], "rw_a2": [2, 64, 1024], "rw_g1": [2, 1024, 160], "rw_g2": [2, 160, 1024],
    "rw_v1": [1, 1024, 32], "rw_v2": [1, 32, 1024], "mlp_w1": [4, 1024, 4096], "mlp_w2": [4, 4096, 1024],
}


def weight_indices(layers):
    w = {"ada_w": list(layers), "mlp_w1": list(layers), "mlp_w2": list(layers)}
    cj = sorted({l // 2 for l in layers if l % 2 == 0})
    rj = sorted({l // 2 for l in layers if l % 2 == 1})
    if cj:
        w["conv_w_in"] = cj
        w["conv_w_out"] = cj
    if rj:
        for n in ("rw_w_rkv", "rw_w_o", "rw_w1", "rw_w2", "rw_a1", "rw_a2", "rw_g1", "rw_g2"):
            w[n] = rj
    if 3 in layers:
        w["rw_v1"] = [0]
        w["rw_v2"] = [0]
    return w


def weights_for(layers):
    return list(weight_indices(layers).keys())


def weight_inputs(inp, layers):
    return {n: np.ascontiguousarray(np.asarray(inp[n], np.float32)[idx]) for n, idx in weight_indices(layers).items()}


def build(layers, fused=False):
    nc = bass.Bass("TRN2", target_bir_lowering=False)
    last = (layers[-1] == DEPTH - 1)
    dr = {}
    dr["xT"] = nc.dram_tensor("xT", [D, T], F32, kind="ExternalInput").ap()
    dr["vecs"] = nc.dram_tensor("vecs", [128, NV], F32, kind="ExternalInput").ap()
    dr["consts"] = nc.dram_tensor("consts", [128, NCONST], F32, kind="ExternalInput").ap()
    widx = weight_indices(layers)
    for n, idx in widx.items():
        dr[n] = nc.dram_tensor(n, [len(idx)] + WEIGHT_SHAPES[n][1:], F32, kind="ExternalInput").ap()

    def wsel(n, i):
        return dr[n][widx[n].index(i)]
    if not fused:
        dr["uh_in"] = nc.dram_tensor("uh_in", [128, 16], F32, kind="ExternalInput").ap()
        dr["hh_in"] = nc.dram_tensor("hh_in", [128, 8], F32, kind="ExternalInput").ap()
        dr["s0_in"] = nc.dram_tensor("s0_in", [128, 512], F32, kind="ExternalInput").ap()
    if 3 in layers and 1 not in layers:
        dr["vf"] = nc.dram_tensor("vf_in", [D, T], F32, kind="ExternalInput").ap()
    elif 1 in layers and 3 not in layers:
        dr["vf"] = nc.dram_tensor("vf_out", [D, T], F32, kind="ExternalOutput").ap()
    elif 1 in layers:
        dr["vf"] = nc.dram_tensor("vf_scr", [D, T], F32).ap()
    dr["outT"] = nc.dram_tensor("outT", [D, T], F32, kind="ExternalOutput").ap()
    if not fused:
        dr["uh_out"] = nc.dram_tensor("uh_out", [128, 16], F32, kind="ExternalOutput").ap()
        dr["hh_out"] = nc.dram_tensor("hh_out", [128, 8], F32, kind="ExternalOutput").ap()
        dr["s_out"] = nc.dram_tensor("s_out", [128, 512], F32, kind="ExternalOutput").ap()
    PAIRS = [[0, 1], [2, 3], [4, 5], [6, 7]]

    with contextlib.ExitStack() as st:
        def sb(name, shape, dt):
            return st.enter_context(nc.sbuf_tensor(name, shape, dt))

        S = Sched(nc)
        xres = sb("xres", [128, KC, T], F32)
        hB = sb("hB", [128, KC * T], BF16)
        waF = sb("waF", [128, 16384], F32)
        waB = waF[:].bitcast(BF16)
        vecs = sb("vecs_sb", [128, NV], F32)
        consts = sb("consts_sb", [128, NCONST], F32)
        cact = sb("cact", [128, 8], F32)
        mod = sb("mod", [128, 48], F32)
        gs = sb("gs", [128, 16], F32)
        onesb = sb("onesb", [128, 128], BF16)
        bones = sb("bones", [128, 128], F32)
        pp = st.enter_context(nc.psum_tensor("pp", [128, 8, 512], F32))
        gen = [sb("gen%d" % i, [128, NT], F32) for i in range(5)]
        t_gen = [Tl("gen%d" % i) for i in range(5)]
        sd, rstd, ntmp = gen[0], gen[1], [gen[2], gen[3]]
        t_sd, t_rstd, t_ntmp = t_gen[0], t_gen[1], [t_gen[2], t_gen[3]]
        sqb = [sb("sqb%d" % i, [128, NT], BF16) for i in range(2)]
        t_sqb = [Tl("sqb0"), Tl("sqb1")]
        PHN = 6144
        PH = sb("PH", [128, PHN], F32)
        ph_state = {"off": 0, "prev": [], "cur": []}

        def ph_begin():
            ph_state["off"] = 0
            ph_state["prev"] = ph_state["cur"]
            ph_state["cur"] = []

        def ph_f32(n, name):
            o = ph_state["off"]
            assert o + n <= PHN, (name, o, n)
            ph_state["off"] = o + n
            t = Tl(name)
            alias_seed([t], ph_state["prev"])
            ph_state["cur"].append(t)
            return PH[:, o:o + n], t

        def ph_bf16(n, name):
            m = (n + 1) // 2
            v, t = ph_f32(m, name)
            return v.bitcast(BF16)[:, 0:n], t

        t_x = [[Tl("x%d_%d" % (c, i)) for i in range(T // NR)] for c in range(KC)]
        t_hB = [Tl("hB%d" % i) for i in range(T // NT)]
        hb_state = {"prev": list(t_hB), "cur": list(t_hB)}

        def hb_begin(new_tls):
            alias_seed(new_tls, hb_state["cur"])
            hb_state["cur"] = list(new_tls)

        t_wa = [Tl("wa0"), Tl("wa1")]
        t_vec, t_con, t_cact, t_mod, t_gs, t_ones = [Tl(n) for n in ("vec", "con", "cact", "mod", "gs", "ones")]
        tb = [Tl("bank%d" % i) for i in range(8)]

        def xt(c, c0, n):
            return [t_x[c][i] for i in range(c0 // NR, (c0 + n + NR - 1) // NR)]

        def xall(c0, n):
            return [t for c in range(KC) for t in xt(c, c0, n)]

        def ACT(out, in_, func, reads, writes, **kw):
            return S.op("act", lambda e: e.activation(out=out, in_=in_, func=func, **kw), reads, writes)

        def TT(eng, out, in0, in1, op, reads, writes):
            return S.op(eng, lambda e: e.tensor_tensor(out=out, in0=in0, in1=in1, op=op), reads, writes)

        def TS(eng, out, in0, s1, s2, op0, op1, reads, writes):
            return S.op(eng, lambda e: e.tensor_scalar(out=out, in0=in0, scalar1=s1, scalar2=s2, op0=op0, op1=op1), reads, writes)

        def STT(out, in0, scalar, in1, op0, op1, reads, writes):
            return S.op("dve", lambda e: e.scalar_tensor_tensor(out=out, in0=in0, scalar=scalar, in1=in1, op0=op0, op1=op1), reads, writes)

        def CP(eng, out, in_, reads, writes):
            return S.op(eng, lambda e: e.tensor_copy(out=out, in_=in_), reads, writes)

        def MM(out, lhsT, rhs, start, stop, reads, writes):
            return S.op("pe", lambda e: e.matmul(out, lhsT, rhs, start=start, stop=stop), reads, writes)

        def vc(col, n=1):
            return vecs[:, col:col + n]

        ex_count = [0]

        def exchange(src_ap, n, src_tls, dst_ap, dst_tls):
            k = ex_count[0]
            ex_count[0] += 1
            ei = nc.dram_tensor("ex_i%d" % k, [128, n], F32)
            eo = nc.dram_tensor("ex_o%d" % k, [256, n], F32)
            t_ei, t_eo = Tl("ei%d" % k), Tl("eo%d" % k)
            S.dma("sp", "exi", lambda e: e.dma_start(out=ei.ap(), in_=src_ap), reads=src_tls, writes=[t_ei])
            S.dma("pool", "cc", lambda e: e.collective_compute("AllGather", ALU.bypass, replica_groups=PAIRS,
                                                              ins=[ei.ap().opt()], outs=[eo.ap().opt()]),
                  reads=[t_ei], writes=[t_eo], inc=1)
            S.dma("sp", "exo", lambda e: e.dma_start(out=dst_ap, in_=eo.ap()[0:128, :]), reads=[t_eo], writes=dst_tls)
            TS("dve", dst_ap, dst_ap, vc(V_HM), None, ALU.mult, ALU.bypass, dst_tls + [t_vec], dst_tls)

        S.dma("sp", "vec", lambda e: e.dma_start(out=vecs[:], in_=dr["vecs"]), writes=[t_vec])
        S.dma("sp", "con", lambda e: e.dma_start(out=consts[:], in_=dr["consts"]), writes=[t_con])
        for c in range(KC):
            S.dma("sp", "xin", lambda e, c=c: e.dma_start(out=xres[:, c, :], in_=dr["xT"][c * 128:(c + 1) * 128, :]),
                  writes=xt(c, 0, T))
        ACT(cact[:], vc(V_C, 8), AF.Silu, [t_vec], [t_cact])
        S.op("pool", lambda e: e.memset(onesb[:], 1.0), writes=[t_ones])
        S.op("pool", lambda e: e.memset(bones[:], 0.0), writes=[t_ones])
        S.op("pool", lambda e: e.memset(bones[0:64, 0:64], 1.0), writes=[t_ones])
        S.op("pool", lambda e: e.memset(bones[64:128, 64:128], 1.0), writes=[t_ones])
        ident = consts[:, C_ID:C_ID + 128]

        def emit_mod(l):
            for jb in range(6):
                slot = jb % 2
                wv = waF[:, slot * 8192:(slot + 1) * 8192].rearrange("p (c n) -> p c n", c=8)
                S.dma("sp" if jb % 2 == 0 else "act", "wa%d" % slot,
                      lambda e, wv=wv, jb=jb: e.dma_start(out=wv, in_=wsel("ada_w", l)[:, jb * 1024:(jb + 1) * 1024].rearrange("(c p) n -> p c n", p=128)),
                      writes=[t_wa[slot]])
                for j in range(8):
                    col = jb * 8 + j
                    for kc in range(KC):
                        MM(pp[:, 7, col:col + 1], wv[:, kc, j * 128:(j + 1) * 128], cact[:, kc:kc + 1], kc == 0, kc == KC - 1,
                           [t_wa[slot], t_cact], [tb[7]])
            TT("dve", mod[:], pp[:, 7, 0:48], vc(V_ADAB + l * 48, 48), ALU.add, [tb[7], t_vec], [t_mod])
            STT(gs[:, 0:8], mod[:, 8:16], 1.0, vc(V_NG + (l * 2) * 8, 8), ALU.add, ALU.mult, [t_mod, t_vec], [t_gs])
            STT(gs[:, 8:16], mod[:, 32:40], 1.0, vc(V_NG + (l * 2 + 1) * 8, 8), ALU.add, ALU.mult, [t_mod, t_vec], [t_gs])

        def emit_rstd(c0, n):
            for c in range(KC):
                k = c % 2
                ACT(sqb[k][:, 0:n], xres[:, c, c0:c0 + n], AF.Square, xt(c, c0, n), [t_sqb[k]])
                MM(pp[:, 7, 0:n], onesb[:], sqb[k][:, 0:n], c == 0, c == KC - 1, [t_ones, t_sqb[k]], [tb[7]])
            ACT(sd[:, 0:n], pp[:, 7, 0:n], AF.Sqrt, [tb[7]], [t_sd], scale=1.0 / D, bias=EPS)
            S.op("dve", lambda e: e.reciprocal(out=rstd[:, 0:n], in_=sd[:, 0:n]), [t_sd], [t_rstd])

        def emit_norm(c0, n, gcol, shcol, outfn, out_tls, out_reads=()):
            emit_rstd(c0, n)
            for c in range(KC):
                if shcol is None:
                    STT(outfn(c), xres[:, c, c0:c0 + n], gcol(c), rstd[:, 0:n], ALU.mult, ALU.mult,
                        xt(c, c0, n) + [t_rstd, t_vec, t_gs] + list(out_reads), out_tls(c))
                else:
                    k = c % 2
                    STT(ntmp[k][:, 0:n], xres[:, c, c0:c0 + n], gcol(c), rstd[:, 0:n], ALU.mult, ALU.mult,
                        xt(c, c0, n) + [t_rstd, t_gs], [t_ntmp[k]])
                    ACT(outfn(c), ntmp[k][:, 0:n], AF.Identity, [t_ntmp[k], t_mod] + list(out_reads), out_tls(c), bias=shcol(c), scale=1.0)

        def hB_view(ti):
            return hB[:].rearrange("p (c t) -> p c t", c=KC)[:, :, ti * NT:(ti + 1) * NT]

        def emit_mlp_norm(ti):
            hv = hB_view(ti)
            emit_norm(ti * NT, NT, lambda c: gs[:, 8 + c:9 + c], lambda c: mod[:, 24 + c:25 + c],
                      lambda c: hv[:, c, :], lambda c: [t_hB[ti]])

        def emit_mlp(l):
            ph_begin()
            hidv, t_hid0 = ph_bf16(KC * NT, "hid")
            hid = hidv.rearrange("p (c t) -> p c t", c=KC)
            t_hid = [t_hid0] + [Tl("hid%d" % m) for m in range(1, KC)]
            for t in t_hid[1:]:
                t.r.extend(t_hid0.r)
                ph_state["cur"].append(t)
            rl = [gen[2], gen[3]]
            t_rl = [t_gen[2], t_gen[3]]
            cnt = 0
            for q in range(4):
                slot = q % 2
                w1v = waB[:, slot * 16384:slot * 16384 + 8192].rearrange("p (c n) -> p c n", c=8)
                w2v = waB[:, slot * 16384 + 8192:(slot + 1) * 16384].rearrange("p (c n) -> p c n", c=8)
                S.dma("pool", "wa%d" % slot,
                      lambda e, w1v=w1v, q=q: e.dma_start(out=w1v, in_=wsel("mlp_w1", l)[:, q * 1024:(q + 1) * 1024].rearrange("(c p) n -> p c n", p=128)),
                      writes=[t_wa[slot]])
                S.dma("pool", "wa%d" % slot,
                      lambda e, w2v=w2v, q=q: e.dma_start(out=w2v, in_=wsel("mlp_w2", l)[q * 1024:(q + 1) * 1024, :].rearrange("(c p) n -> p c n", p=128)),
                      writes=[t_wa[slot]])
                for ti in range(T // NT):
                    hv = hB_view(ti)
                    for m in range(KC):
                        bk = m % 4
                        for kc in range(KC):
                            MM(pp[:, bk, :], w1v[:, kc, m * 128:(m + 1) * 128], hv[:, kc, :], kc == 0, kc == KC - 1,
                               [t_wa[slot], t_hB[ti]], [tb[bk]])
                        k = cnt % 2
                        cnt += 1
                        ACT(rl[k][:], pp[:, bk, :], AF.Relu, [tb[bk]], [t_rl[k]])
                        TT("pool", hid[:, m, :], rl[k][:], rl[k][:], ALU.mult, [t_rl[k]], [t_hid[m]])
                    for oc in range(KC):
                        bk = 4 + oc % 4
                        for m in range(KC):
                            MM(pp[:, bk, :], w2v[:, m, oc * 128:(oc + 1) * 128], hid[:, m, :], m == 0, m == KC - 1,
                               [t_wa[slot], t_hid[m]], [tb[bk]])
                        xs = xres[:, oc, ti * NT:(ti + 1) * NT]
                        STT(xs, pp[:, bk, :], mod[:, 40 + oc:41 + oc], xs, ALU.mult, ALU.add,
                            [tb[bk], t_mod] + xt(oc, ti * NT, NT), xt(oc, ti * NT, NT))

        def emit_conv(l):
            j = l // 2
            ph_begin()
            winv = waB[:, 0:24576].rearrange("p (c n) -> p c n", c=8)
            woutv = waB[:, 24576:32768].rearrange("p (c n) -> p c n", c=8)
            for part in range(3):
                S.dma("pool", "wa0", lambda e, part=part: e.dma_start(
                    out=winv[:, :, part * 1024:(part + 1) * 1024],
                    in_=wsel("conv_w_in", j)[:, part * 1024:(part + 1) * 1024].rearrange("(c p) n -> p c n", p=128)),
                    writes=[t_wa[0], t_wa[1]])
            S.dma("pool", "wa1", lambda e: e.dma_start(out=woutv, in_=wsel("conv_w_out", j).rearrange("(c p) n -> p c n", p=128)),
                  writes=[t_wa[1]])
            hc = hB[:, 0:KC * NT].rearrange("p (c t) -> p c t", c=KC)
            G = hB[:, KC * NT:2 * KC * NT].rearrange("p (c t) -> p c t", c=KC)
            t_hc = [Tl("hc%d" % c) for c in range(KC)]
            t_G = [Tl("G%d" % c) for c in range(KC)]
            hb_begin(t_hc + t_G)
            Uv, t_U0 = ph_f32(KC * (NT + 2), "U")
            U = Uv.rearrange("p (c t) -> p c t", c=KC)
            t_U = [Tl("U%d" % c) for c in range(KC)]
            t_Uh = Tl("Uh")
            for t in t_U + [t_Uh]:
                t.r.extend(t_U0.r)
                ph_state["cur"].append(t)
            uhf, t_uhf = ph_f32(16, "uhf")
            uht, t_uht = ph_f32(16, "uht")
            cs, t_cs = gen[4], t_gen[4]
            ca, t_ca = [gen[2], gen[3]], [t_gen[2], t_gen[3]]
            if not fused:
                S.dma("sp", "halo", lambda e: e.dma_start(out=uhf, in_=dr["uh_in"]), writes=[t_uhf])
            else:
                htl, t_htl = ph_bf16(KC * 2, "htl")
                htl = htl.rearrange("p (c t) -> p c t", c=KC)
                emit_norm(T - 2, 2, lambda c: gs[:, c:c + 1], lambda c: mod[:, c:c + 1],
                          lambda c: htl[:, c, :], lambda c: [t_htl])
                uht3 = uht.rearrange("p (c t) -> p c t", t=2)
                for fc in range(KC):
                    for gi, off in enumerate((1024, 2048)):
                        for kc in range(KC):
                            MM(pp[:, gi, 0:2], winv[:, kc, off + fc * 128:off + (fc + 1) * 128], htl[:, kc, :], kc == 0, kc == KC - 1,
                               [t_wa[0], t_wa[1], t_htl], [tb[gi]])
                    ACT(cs[:, 0:2], pp[:, 0, 0:2], AF.Copy, [tb[0]], [t_cs])
                    TT("dve", uht3[:, fc, :], cs[:, 0:2], pp[:, 1, 0:2], ALU.mult, [t_cs, tb[1]], [t_uht])
                exchange(uht, 16, [t_uht], uhf, [t_uhf])
            for ti in range(T // NT):
                c0 = ti * NT
                if ti == 0:
                    CP("pool", U[:, :, 0:2], uhf.rearrange("p (c t) -> p c t", t=2), [t_uhf], [t_Uh])
                else:
                    CP("pool", U[:, :, 0:2], U[:, :, NT:NT + 2], t_U, [t_Uh])
                emit_norm(c0, NT, lambda c: gs[:, c:c + 1], lambda c: mod[:, c:c + 1],
                          lambda c: hc[:, c, :], lambda c: [t_hc[c]])
                for fc in range(KC):
                    for gi, off in enumerate((1024, 2048, 0)):
                        for kc in range(KC):
                            MM(pp[:, gi, :], winv[:, kc, off + fc * 128:off + (fc + 1) * 128], hc[:, kc, :], kc == 0, kc == KC - 1,
                               [t_wa[0], t_wa[1], t_hc[kc]], [tb[gi]])
                    ACT(cs[:], pp[:, 0, :], AF.Copy, [tb[0]], [t_cs])
                    TT("dve", U[:, fc, 2:NT + 2], cs[:], pp[:, 1, :], ALU.mult, [t_cs, tb[1]], [t_U[fc]])
                    k = fc % 2
                    cwb = V_CONVW + (j * 3) * 8
                    ACT(ca[k][:], U[:, fc, 2:NT + 2], AF.Copy, [t_U[fc], t_vec], [t_ca[k]], scale=vc(cwb + 2 * 8 + fc))
                    STT(ca[k][:], U[:, fc, 1:NT + 1], vc(cwb + 1 * 8 + fc), ca[k][:], ALU.mult, ALU.add, [t_U[fc], t_Uh, t_ca[k], t_vec], [t_ca[k]])
                    STT(ca[k][:], U[:, fc, 0:NT], vc(cwb + 0 * 8 + fc), ca[k][:], ALU.mult, ALU.add, [t_U[fc], t_Uh, t_ca[k], t_vec], [t_ca[k]])
                    TT("dve", G[:, fc, :], ca[k][:], pp[:, 2, :], ALU.mult, [t_ca[k], tb[2]], [t_G[fc]])
                for oc in range(KC):
                    bk = 3 + oc % 2
                    for kc in range(KC):
                        MM(pp[:, bk, :], woutv[:, kc, oc * 128:(oc + 1) * 128], G[:, kc, :], kc == 0, kc == KC - 1,
                           [t_wa[1], t_G[kc]], [tb[bk]])
                    xs = xres[:, oc, c0:c0 + NT]
                    STT(xs, pp[:, bk, :], mod[:, 16 + oc:17 + oc], xs, ALU.mult, ALU.add,
                        [tb[bk], t_mod] + xt(oc, c0, NT), xt(oc, c0, NT))
            tk = None
            if not fused:
                CP("pool", uht.rearrange("p (c t) -> p c t", t=2), U[:, :, NT:NT + 2], t_U, [t_uht])
                tk = S.dma("sp", "tails", lambda e: e.dma_start(out=dr["uh_out"], in_=uht), reads=[t_uht])
            hb_begin(t_hB)
            for ti in range(T // NT):
                emit_mlp_norm(ti)
            return tk

        def emit_rwkv(l):
            j = l // 2
            vres = (j > 0)
            ph_begin()
            wr = waB[:, 0:8192].rearrange("p (c n) -> p c n", c=8)
            wk = waB[:, 8192:16384].rearrange("p (c n) -> p c n", c=8)
            wv = waB[:, 16384:24576].rearrange("p (c n) -> p c n", c=8)
            lo = [24576]

            def acarve(n):
                v = waB[:, lo[0]:lo[0] + n]
                lo[0] += n
                assert lo[0] <= 32768
                return v
            w1b = acarve(KC * 64).rearrange("p (c n) -> p c n", c=KC)
            a1b = acarve(KC * 64).rearrange("p (c n) -> p c n", c=KC)
            g1b = acarve(KC * 160).rearrange("p (c n) -> p c n", c=KC)
            w2b = acarve(D)
            a2b = acarve(D)
            g2b = acarve(2 * D).rearrange("p (c n) -> p c n", c=2)
            if vres:
                v1b = acarve(KC * 32).rearrange("p (c n) -> p c n", c=KC)
                v2b = acarve(D)
            t_lw = t_wa[1]
            for i, dst in enumerate((wr, wk)):
                S.dma("pool", "wa0", lambda e, i=i, dst=dst: e.dma_start(out=dst, in_=wsel("rw_w_rkv", j)[i].rearrange("(c p) n -> p c n", p=128)), writes=[t_wa[0]])
            S.dma("pool", "wa1", lambda e: e.dma_start(out=wv, in_=wsel("rw_w_rkv", j)[2].rearrange("(c p) n -> p c n", p=128)), writes=[t_wa[1]])
            S.dma("pool", "wa1", lambda e: e.dma_start(out=w1b, in_=wsel("rw_w1", j).rearrange("(c p) n -> p c n", p=128)), writes=[t_lw])
            S.dma("pool", "wa1", lambda e: e.dma_start(out=a1b, in_=wsel("rw_a1", j).rearrange("(c p) n -> p c n", p=128)), writes=[t_lw])
            S.dma("pool", "wa1", lambda e: e.dma_start(out=g1b, in_=wsel("rw_g1", j).rearrange("(c p) n -> p c n", p=128)), writes=[t_lw])
            S.dma("pool", "wa1", lambda e: e.dma_start(out=w2b[0:64, :], in_=wsel("rw_w2", j)), writes=[t_lw])
            S.dma("pool", "wa1", lambda e: e.dma_start(out=a2b[0:64, :], in_=wsel("rw_a2", j)), writes=[t_lw])
            S.dma("pool", "wa1", lambda e: e.dma_start(out=g2b[:, 0, :], in_=wsel("rw_g2", j)[0:128, :]), writes=[t_lw])
            S.dma("pool", "wa1", lambda e: e.dma_start(out=g2b[0:32, 1, :], in_=wsel("rw_g2", j)[128:160, :]), writes=[t_lw])
            if vres:
                S.dma("pool", "wa1", lambda e: e.dma_start(out=v1b, in_=wsel("rw_v1", j - 1).rearrange("(c p) n -> p c n", p=128)), writes=[t_lw])
                S.dma("pool", "wa1", lambda e: e.dma_start(out=v2b[0:32, :], in_=wsel("rw_v2", j - 1)), writes=[t_lw])

            HTW = NR + 8
            ho = [0]

            def hcarve(n):
                v = hB[:, ho[0]:ho[0] + n]
                ho[0] += n
                assert ho[0] <= KC * T
                return v
            hT = hcarve(KC * HTW).rearrange("p (c t) -> p c t", c=KC)
            xx, xr, xk, xv, xm, YG = [hcarve(KC * NR).rearrange("p (c t) -> p c t", c=KC) for _ in range(6)]
            wo = hcarve(KC * D).rearrange("p (c n) -> p c n", c=KC)
            t_hT, t_xx, t_xr, t_xk, t_xv, t_xm, t_wo = [Tl(n) for n in ("hT", "xx", "xr", "xk", "xv", "xm", "wo")]
            t_YG = [Tl("YG%d" % c) for c in range(KC)]
            hb_begin([t_hT, t_xx, t_xr, t_xk, t_xv, t_xm, t_wo] + t_YG)
            S.dma("pool", "wo", lambda e: e.dma_start(out=wo, in_=wsel("rw_w_o", j).rearrange("(c p) n -> p c n", p=128)), writes=[t_wo])

            def f32t(name, n=NR):
                return ph_f32(n, name)
            hw, t_hw = ph_bf16(NR, "hw"); ha, t_ha = ph_bf16(NR, "ha")
            hg0, t_hg = ph_bf16(NR, "hg0"); hg1, t_hg1 = ph_bf16(NR, "hg1")
            if vres:
                hvv, t_hv = ph_bf16(NR, "hvv")
                vft, t_vft = f32t("vft")
                vg, t_vg = f32t("vg")
            hhf, t_hhf = f32t("hhf", 8)
            hht, t_hht = f32t("hht", 8)
            if not fused:
                S.dma("sp", "halo", lambda e: e.dma_start(out=hhf, in_=dr["hh_in"]), writes=[t_hhf])
            STv, t_ST0 = f32t("ST", KC * 64)
            ST = STv.rearrange("p (c v) -> p c v", c=KC)
            t_ST = [Tl("ST%d" % c) for c in range(KC)]
            for t in t_ST:
                t.r.extend(t_ST0.r)
                ph_state["cur"].append(t)
            if not fused:
                S.dma("sp", "halo", lambda e: e.dma_start(out=STv, in_=dr["s0_in"]), writes=t_ST)
            else:
                S.op("pool", lambda e: e.memset(STv, 0.0), writes=t_ST)
            smask, t_smask = f32t("smask")
            S.op("pool", lambda e: e.memset(smask, 1.0), writes=[t_smask])
            S.op("pool", lambda e: e.memset(smask.rearrange("p (c l) -> p c l", l=LC)[:, :, 0:1], 0.0), writes=[t_smask])
            names = ["s", "cl", "dd", "Wt", "Wm", "iW", "a", "ksb", "kk", "sqk", "rn", "t1", "kmod", "beta", "vsb", "rsb", "bon", "gsb"]
            W = {}
            tw = {}
            for n in names:
                W[n], tw[n] = f32t(n)
            for new, old in (("rk", "sqk"), ("ysb", "s"), ("ysq", "cl"), ("yc", "dd"), ("m2", "Wm"), ("var", "iW"), ("rs", "a")):
                W[new], tw[new] = W[old], tw[old]
            ARv, t_AR = f32t("AR", NCH * 128); AR = ARv.rearrange("p (c n) -> p c n", c=NCH)
            BKv, t_BK = f32t("BK", NCH * 128); BK = BKv.rearrange("p (c n) -> p c n", c=NCH)
            KHv, t_KH = f32t("KH", NCH * 64); KH = KHv.rearrange("p (c n) -> p c n", c=NCH)
            BHv, t_BH = f32t("BH", NCH * 64); BH = BHv.rearrange("p (c n) -> p c n", c=NCH)
            TM, t_TM, AM, t_AM, PTQ, t_PTQ, Tf, t_Tf = [], [], [], [], [], [], [], []
            for i in range(2):
                v, t = f32t("TM%d" % i, 192); TM.append(v.rearrange("p (a b) -> p a b", a=3)); t_TM.append(t)
                v, t = f32t("AM%d" % i, 320); AM.append(v.rearrange("p (a b) -> p a b", a=5)); t_AM.append(t)
                v, t = f32t("PTQ%d" % i, 192); PTQ.append(v); t_PTQ.append(t)
                v, t = f32t("Tf%d" % i, 64); Tf.append(v); t_Tf.append(t)
            XT, t_XT = f32t("XT", 64)
            UT, t_UT = f32t("UT", 64)
            M5 = consts[:, C_M5:C_M5 + 320]
            ID2 = consts[:, C_ID2:C_ID2 + 64]
            nchunk_global = [0]
            v3 = lambda ap: ap.rearrange("p (c l) -> p c l", l=LC)
            S.op("pool", lambda e: e.memset(ARv, 0.0), writes=[t_AR])

            def run_pass(so):
              for ti in range(T // NR):
                  c0 = ti * NR
                  if ti == 0:
                      CP("dve", hT[:, :, 0:1], hhf.rearrange("p (c t) -> p c t", t=1), [t_hhf], [t_hT])
                  else:
                      CP("dve", hT[:, :, 0:1], hT[:, :, NR:NR + 1], [t_hT], [t_hT])
                  emit_norm(c0, NR, lambda c: gs[:, c:c + 1], lambda c: mod[:, c:c + 1],
                            lambda c: hT[:, c, 1:NR + 1], lambda c: [t_hT])
                  TT("dve", xx, hT[:, :, 0:NR], hT[:, :, 1:NR + 1], ALU.subtract, [t_hT], [t_xx])

                  def mix(i, dst, t_dst):
                      for c in range(KC):
                          STT(dst[:, c, :], xx[:, c, :], vc(V_MU + (j * 6 + i) * 8 + c), hT[:, c, 1:NR + 1], ALU.mult, ALU.add,
                              [t_xx, t_hT, t_vec], [t_dst])
                  mix(1, xm, t_xm)
                  for kc in range(KC):
                      MM(pp[0:64, 0, 0:NR], w1b[:, kc, :], xm[:, kc, :], kc == 0, kc == KC - 1, [t_lw, t_xm], [tb[0]])
                  ACT(hw[0:64, :], pp[0:64, 0, 0:NR], AF.Tanh, [tb[0]], [t_hw])
                  mix(4, xm, t_xm)
                  for kc in range(KC):
                      MM(pp[0:64, 1, 0:NR], a1b[:, kc, :], xm[:, kc, :], kc == 0, kc == KC - 1, [t_lw, t_xm], [tb[1]])
                  ACT(ha[0:64, :], pp[0:64, 1, 0:NR], AF.Copy, [tb[1]], [t_ha])
                  if not so:
                      mix(5, xm, t_xm)
                      for kc in range(KC):
                          MM(pp[:, 0, 0:NR], g1b[:, kc, 0:128], xm[:, kc, :], kc == 0, kc == KC - 1, [t_lw, t_xm], [tb[0]])
                      for kc in range(KC):
                          MM(pp[0:32, 1, 0:NR], g1b[:, kc, 128:160], xm[:, kc, :], kc == 0, kc == KC - 1, [t_lw, t_xm], [tb[1]])
                      ACT(hg0, pp[:, 0, 0:NR], AF.Sigmoid, [tb[0]], [t_hg])
                      ACT(hg1[0:32, :], pp[0:32, 1, 0:NR], AF.Sigmoid, [tb[1]], [t_hg1])
                      mix(0, xr, t_xr)
                  mix(2, xk, t_xk)
                  mix(3, xv, t_xv)
                  if vres:
                      for kc in range(KC):
                          MM(pp[0:32, 0, 0:NR], v1b[:, kc, :], xv[:, kc, :], kc == 0, kc == KC - 1, [t_lw, t_xv], [tb[0]])
                      ACT(hvv[0:32, :], pp[0:32, 0, 0:NR], AF.Copy, [tb[0]], [t_hv])

                  for fc in range(KC):
                      fs = slice(fc * 128, (fc + 1) * 128)
                      vcol = lambda base, fc=fc: vc(base + j * 8 + fc)

                      def proj(bank, wmat, xin, t_xin, twa, fs=fs):
                          for kc in range(KC):
                              MM(pp[:, bank, 0:NR], wmat[:, kc, fs], xin[:, kc, :], kc == 0, kc == KC - 1, [twa, t_xin], [tb[bank]])
                      MM(pp[:, 0, 0:NR], w2b[0:64, fs], hw[0:64, :], True, True, [t_lw, t_hw], [tb[0]])
                      ACT(W["s"], pp[:, 0, 0:NR], AF.Sigmoid, [tb[0], t_vec], [tw["s"]], bias=vcol(V_W0), scale=1.0)
                      S.op("dve", lambda e: e.tensor_tensor_scan(out=W["cl"], data0=smask, data1=W["s"], initial=0.0, op0=ALU.mult, op1=ALU.add),
                           [t_smask, tw["s"]], [tw["cl"]])
                      TT("pool", W["dd"], W["cl"], W["s"], ALU.subtract, [tw["cl"], tw["s"]], [tw["dd"]])
                      ACT(W["Wt"], W["cl"], AF.Exp, [tw["cl"]], [tw["Wt"]], scale=-CW)
                      ACT(W["Wm"], W["dd"], AF.Exp, [tw["dd"]], [tw["Wm"]], scale=-CW)
                      ACT(W["iW"], W["cl"], AF.Exp, [tw["cl"]], [tw["iW"]], scale=CW)
                      MM(pp[:, 1, 0:NR], a2b[0:64, fs], ha[0:64, :], True, True, [t_lw, t_ha], [tb[1]])
                      ACT(W["a"], pp[:, 1, 0:NR], AF.Sigmoid, [tb[1], t_vec], [tw["a"]], bias=vcol(V_A0), scale=1.0)
                      proj(0, wk, xk, t_xk, t_wa[0])
                      ACT(W["ksb"], pp[:, 0, 0:NR], AF.Copy, [tb[0]], [tw["ksb"]])
                      TS("pool", W["kk"], W["ksb"], vcol(V_KK), None, ALU.mult, ALU.bypass, [tw["ksb"], t_vec], [tw["kk"]])
                      TT("pool", W["sqk"], W["kk"], W["kk"], ALU.mult, [tw["kk"]], [tw["sqk"]])
                      MM(pp[:, 7, 0:NR], bones[:], W["sqk"], True, True, [t_ones, tw["sqk"]], [tb[7]])
                      ACT(W["rn"], pp[:, 7, 0:NR], AF.Sqrt, [tb[7]], [tw["rn"]], bias=1e-30, scale=1.0)
                      S.op("dve", lambda e: e.reciprocal(out=W["rn"], in_=W["rn"]), [tw["rn"]], [tw["rn"]])
                      TT("pool", W["kk"], W["kk"], W["rn"], ALU.mult, [tw["kk"], tw["rn"]], [tw["kk"]])
                      TS("pool", W["t1"], W["a"], -1.0, vcol(V_KA), ALU.add, ALU.mult, [tw["a"], t_vec], [tw["t1"]])
                      STT(W["kmod"], W["t1"], 1.0, W["ksb"], ALU.add, ALU.mult, [tw["t1"], tw["ksb"]], [tw["kmod"]])
                      TT("pool", W["beta"], W["kk"], W["a"], ALU.mult, [tw["kk"], tw["a"]], [tw["beta"]])
                      if not so:
                          proj(1, wr, xr, t_xr, t_wa[0])
                          ACT(W["rsb"], pp[:, 1, 0:NR], AF.Copy, [tb[1]], [tw["rsb"]])
                          TT("dve", AR[:, :, 64:128], v3(W["rsb"]), v3(W["Wt"]), ALU.mult, [tw["rsb"], tw["Wt"]], [t_AR])
                      STT(AR[:, :, 0:64], v3(W["kk"]), -1.0, v3(W["Wm"]), ALU.mult, ALU.mult, [tw["kk"], tw["Wm"]], [t_AR])
                      TT("pool", BK[:, :, 0:64], v3(W["beta"]), v3(W["iW"]), ALU.mult, [tw["beta"], tw["iW"]], [t_BK])
                      TT("pool", BK[:, :, 64:128], v3(W["kmod"]), v3(W["iW"]), ALU.mult, [tw["kmod"], tw["iW"]], [t_BK])
                      WL = v3(W["Wt"])[:, :, 63:64].to_broadcast([128, NCH, 64])
                      TT("dve", KH, BK[:, :, 64:128], WL, ALU.mult, [t_BK, tw["Wt"]], [t_KH])
                      TT("dve", BH, BK[:, :, 0:64], WL, ALU.mult, [t_BK, tw["Wt"]], [t_BH])
                      proj(2, wv, xv, t_xv, t_wa[1])
                      ACT(W["vsb"], pp[:, 2, 0:NR], AF.Copy, [tb[2]], [tw["vsb"]])
                      if not vres:
                          if not so:
                              S.dma("sp", "vf", lambda e, fs=fs, c0=c0: e.dma_start(out=dr["vf"][fs, c0:c0 + NR], in_=W["vsb"]), reads=[tw["vsb"]])
                      else:
                          S.dma("sp", "vf", lambda e, fs=fs, c0=c0: e.dma_start(out=vft, in_=dr["vf"][fs, c0:c0 + NR]), writes=[t_vft])
                          MM(pp[:, 2, 0:NR], v2b[0:32, fs], hvv[0:32, :], True, True, [t_lw, t_hv], [tb[2]])
                          ACT(vg, pp[:, 2, 0:NR], AF.Sigmoid, [tb[2], t_vec], [t_vg], bias=vc(V_V0 + fc), scale=1.0)
                          TT("pool", vft, vft, W["vsb"], ALU.subtract, [t_vft, tw["vsb"]], [t_vft])
                          TT("pool", vft, vft, vg, ALU.mult, [t_vft, t_vg], [t_vft])
                          TT("pool", W["vsb"], W["vsb"], vft, ALU.add, [tw["vsb"], t_vft], [tw["vsb"]])
                      if not so:
                          STT(W["rk"], W["rsb"], vcol(V_RK), W["kmod"], ALU.mult, ALU.mult, [tw["rsb"], tw["kmod"], t_vec], [tw["rk"]])
                          MM(pp[:, 7, 0:NR], bones[:], W["rk"], True, True, [t_ones, tw["rk"]], [tb[7]])
                          TT("dve", W["bon"], W["vsb"], pp[:, 7, 0:NR], ALU.mult, [tw["vsb"], tb[7]], [tw["bon"]])
                          MM(pp[:, 0, 0:NR], g2b[:, 0, fs], hg0, True, False, [t_lw, t_hg], [tb[0]])
                          MM(pp[:, 0, 0:NR], g2b[0:32, 1, fs], hg1[0:32, :], False, True, [t_lw, t_hg1], [tb[0]])
                          ACT(W["gsb"], pp[:, 0, 0:NR], AF.Copy, [tb[0]], [tw["gsb"]])

                      for c in range(NCH):
                          par = nchunk_global[0] % 2
                          nchunk_global[0] += 1
                          cs_ = slice(c * LC, (c + 1) * LC)
                          for h in range(2):
                              hs = slice(64 * h, 64 * h + 64)
                              idb = consts[hs, C_ID + 64 * h:C_ID + 64 * h + 64]
                              MM(pp[hs, 6, 0:64], W["vsb"][hs, cs_], idb, True, True, [tw["vsb"], t_con], [tb[6]])
                              MM(pp[hs, 6, 64:128], KH[hs, c, :], idb, True, True, [t_KH, t_con], [tb[6]])
                              MM(pp[hs, 6, 128:192], BH[hs, c, :], idb, True, True, [t_BH, t_con], [tb[6]])
                          tm = TM[par]
                          ACT(tm.rearrange("p a b -> p (a b)"), pp[:, 6, 0:192], AF.Copy, [tb[6]], [t_TM[par]])
                          for h in range(2):
                              hs = slice(64 * h, 64 * h + 64)
                              MM(pp[hs, 2, 0:128], BK[hs, c, 0:64], AR[hs, c, :], True, True, [t_BK, t_AR], [tb[2]])
                              MM(pp[hs, 2, 128:192], AR[hs, c, 0:64], BK[hs, c, 0:64], True, True, [t_BK, t_AR], [tb[2]])
                              MM(pp[hs, 2, 192:320], BK[hs, c, 64:128], AR[hs, c, :], True, True, [t_BK, t_AR], [tb[2]])
                          am = AM[par]
                          TT("dve", am.rearrange("p a b -> p (a b)"), pp[:, 2, 0:320], M5, ALU.mult, [tb[2], t_con], [t_AM[par]])
                          TT("dve", PTQ[0][:, 128:192], am[:, 0, :], ID2, ALU.add, [t_AM[par], t_con], [t_PTQ[0]])
                          for h in range(2):
                              hs = slice(64 * h, 64 * h + 64)
                              MM(pp[hs, 3, 64:128], am[hs, 2, :], am[hs, 0, :], True, True, [t_AM[par]], [tb[3]])
                              MM(pp[hs, 3, 0:64], am[hs, 0, :], am[hs, 2, :], True, True, [t_AM[par]], [tb[3]])
                          ACT(PTQ[0][:, 0:128], pp[:, 3, 0:128], AF.Copy, [tb[3]], [t_PTQ[0]])
                          cur = 0
                          for stp in range(1, 6):
                              nxt = 1 - cur
                              lastst = (stp == 5)
                              for h in range(2):
                                  hs = slice(64 * h, 64 * h + 64)
                                  if not lastst:
                                      MM(pp[hs, 3, 64:192], PTQ[cur][hs, 0:64], PTQ[cur][hs, 64:192], True, True, [t_PTQ[cur]], [tb[3]])
                                      MM(pp[hs, 3, 0:64], PTQ[cur][hs, 64:128], PTQ[cur][hs, 0:64], True, True, [t_PTQ[cur]], [tb[3]])
                                  else:
                                      MM(pp[hs, 3, 128:192], PTQ[cur][hs, 0:64], PTQ[cur][hs, 128:192], True, True, [t_PTQ[cur]], [tb[3]])
                              if not lastst:
                                  ACT(PTQ[nxt][:, 0:128], pp[:, 3, 0:128], AF.Copy, [tb[3]], [t_PTQ[nxt]])
                                  TT("dve", PTQ[nxt][:, 128:192], PTQ[cur][:, 128:192], pp[:, 3, 128:192], ALU.add, [t_PTQ[cur], tb[3]], [t_PTQ[nxt]])
                                  cur = nxt
                              else:
                                  TT("dve", Tf[par], PTQ[cur][:, 128:192], pp[:, 3, 128:192], ALU.add, [t_PTQ[cur], tb[3]], [t_Tf[par]])
                          for h in range(2):
                              hs = slice(64 * h, 64 * h + 64)
                              MM(pp[hs, 4, 0:64], AR[hs, c, 0:64], ST[hs, fc, :], True, False, [t_AR, t_ST[fc]], [tb[4]])
                              MM(pp[hs, 4, 0:64], am[hs, 3, :], tm[hs, 0, :], False, True, [t_AM[par], t_TM[par]], [tb[4]])
                          ACT(XT, pp[:, 4, 0:64], AF.Copy, [tb[4]], [t_XT])
                          for h in range(2):
                              hs = slice(64 * h, 64 * h + 64)
                              MM(pp[hs, 4, 64:128], Tf[par][hs, :], XT[hs, :], True, True, [t_Tf[par], t_XT], [tb[4]])
                          ACT(UT, pp[:, 4, 64:128], AF.Copy, [tb[4]], [t_UT])
                          for h in range(0 if not so else 2, 2):
                              hs = slice(64 * h, 64 * h + 64)
                              MM(pp[hs, 5, cs_], ST[hs, fc, :], AR[hs, c, 64:128], True, False, [t_ST[fc], t_AR], [tb[5]])
                              MM(pp[hs, 5, cs_], UT[hs, :], am[hs, 1, :], False, False, [t_UT, t_AM[par]], [tb[5]])
                              MM(pp[hs, 5, cs_], tm[hs, 0, :], am[hs, 4, :], False, True, [t_TM[par], t_AM[par]], [tb[5]])
                          for h in range(2):
                              hs = slice(64 * h, 64 * h + 64)
                              MM(pp[hs, 4, 128:192], tm[hs, 2, :], UT[hs, :], True, False, [t_TM[par], t_UT], [tb[4]])
                              MM(pp[hs, 4, 128:192], tm[hs, 1, :], tm[hs, 0, :], False, True, [t_TM[par]], [tb[4]])
                          STT(ST[:, fc, :], ST[:, fc, :], W["Wt"][:, c * LC + 63:c * LC + 64], pp[:, 4, 128:192], ALU.mult, ALU.add,
                              [t_ST[fc], tw["Wt"], tb[4]], [t_ST[fc]])
                      if so:
                          continue
                      ACT(W["ysb"], pp[:, 5, 0:NR], AF.Copy, [tb[5]], [tw["ysb"]])
                      TT("pool", W["ysq"], W["ysb"], W["ysb"], ALU.mult, [tw["ysb"]], [tw["ysq"]])
                      MM(pp[:, 7, 0:NR], bones[:], W["ysb"], True, True, [t_ones, tw["ysb"]], [tb[7]])
                      MM(pp[:, 6, 0:NR], bones[:], W["ysq"], True, True, [t_ones, tw["ysq"]], [tb[6]])
                      STT(W["yc"], pp[:, 7, 0:NR], -1.0 / 64, W["ysb"], ALU.mult, ALU.add, [tb[7], tw["ysb"]], [tw["yc"]])
                      ACT(W["m2"], pp[:, 7, 0:NR], AF.Square, [tb[7]], [tw["m2"]], scale=1.0 / 64)
                      STT(W["var"], pp[:, 6, 0:NR], 1.0 / 64, W["m2"], ALU.mult, ALU.subtract, [tb[6], tw["m2"]], [tw["var"]])
                      ACT(W["rs"], W["var"], AF.Sqrt, [tw["var"]], [tw["rs"]], bias=GN_EPS, scale=1.0)
                      S.op("dve", lambda e: e.reciprocal(out=W["rs"], in_=W["rs"]), [tw["rs"]], [tw["rs"]])
                      TT("pool", W["yc"], W["yc"], W["rs"], ALU.mult, [tw["yc"], tw["rs"]], [tw["yc"]])
                      ACT(W["yc"], W["yc"], AF.Identity, [tw["yc"], t_vec], [tw["yc"]], bias=vcol(V_LNB), scale=vcol(V_LNW))
                      TT("pool", W["yc"], W["yc"], W["bon"], ALU.add, [tw["yc"], tw["bon"]], [tw["yc"]])
                      TT("pool", YG[:, fc, :], W["yc"], W["gsb"], ALU.mult, [tw["yc"], tw["gsb"]], [t_YG[fc]])
                  for oc in range(KC if not so else 0):
                      bk = oc % 2
                      for kc in range(KC):
                          MM(pp[:, bk, 0:NR], wo[:, kc, oc * 128:(oc + 1) * 128], YG[:, kc, :], kc == 0, kc == KC - 1, [t_wo, t_YG[kc]], [tb[bk]])
                      xs = xres[:, oc, c0:c0 + NR]
                      STT(xs, pp[:, bk, 0:NR], mod[:, 16 + oc:17 + oc], xs, ALU.mult, ALU.add, [tb[bk], t_mod] + xt(oc, c0, NR), xt(oc, c0, NR))
            tk = None
            if not fused:
                run_pass(False)
                CP("dve", hht.rearrange("p (c t) -> p c t", t=1), hT[:, :, NR:NR + 1], [t_hT], [t_hht])
                S.dma("sp", "tails", lambda e: e.dma_start(out=dr["hh_out"], in_=hht), reads=[t_hht])
                tk = S.dma("sp", "tails", lambda e: e.dma_start(out=dr["s_out"], in_=STv), reads=t_ST)
            else:
                htl, t_htl = ph_bf16(KC * 2, "htl")
                htl = htl.rearrange("p (c t) -> p c t", c=KC)
                emit_norm(T - 2, 2, lambda c: gs[:, c:c + 1], lambda c: mod[:, c:c + 1],
                          lambda c: htl[:, c, :], lambda c: [t_htl])
                CP("dve", hht.rearrange("p (c t) -> p c t", t=1), htl[:, :, 1:2], [t_htl], [t_hht])
                exchange(hht, 8, [t_hht], hhf, [t_hhf])
                run_pass(True)
                exchange(STv, KC * 64, t_ST, STv, t_ST)
                run_pass(False)
            hb_begin(t_hB)
            for ti in range(T // NT):
                emit_mlp_norm(ti)
            return tk

        finals = []
        for l in layers:
            emit_mod(l)
            tk = emit_conv(l) if l % 2 == 0 else emit_rwkv(l)
            if tk is not None:
                finals.append(tk)
            emit_mlp(l)
        ost = [gen[2], gen[3]]
        t_ost = [t_gen[2], t_gen[3]]
        if last:
            cnt = 0
            for ti in range(T // NT):
                c0 = ti * NT
                emit_rstd(c0, NT)
                for c in range(KC):
                    k = cnt % 2
                    cnt += 1
                    STT(ost[k][:], xres[:, c, c0:c0 + NT], vc(V_FG + c), rstd[:, 0:NT], ALU.mult, ALU.mult,
                        xt(c, c0, NT) + [t_rstd, t_vec], [t_ost[k]])
                    finals.append(S.dma("sp", "out%d" % k, lambda e, k=k, c=c, c0=c0: e.dma_start(out=dr["outT"][c * 128:(c + 1) * 128, c0:c0 + NT], in_=ost[k][:]),
                                        reads=[t_ost[k]]))
        else:
            for c in range(KC):
                finals.append(S.dma("sp", "out0", lambda e, c=c: e.dma_start(out=dr["outT"][c * 128:(c + 1) * 128, :], in_=xres[:, c, :]),
                                    reads=xt(c, 0, T)))
        S.emit(final_waits=finals)
    return nc


def fm(v):
    v = np.asarray(v, np.float32).reshape(-1, 8, 128)
    return np.ascontiguousarray(v.transpose(2, 0, 1).reshape(128, -1))


def make_consts():
    c = np.zeros((128, NCONST), np.float32)
    c[:, C_ID:C_ID + 128] = np.eye(128, dtype=np.float32)
    p = np.arange(128)[:, None] % 64
    f = np.arange(64)[None, :]
    strict = (p < f).astype(np.float32)
    incl = (p <= f).astype(np.float32)
    lower = (f < p).astype(np.float32)
    m5 = np.stack([strict, incl, lower, strict, incl], axis=1)
    c[:, C_M5:C_M5 + 320] = m5.reshape(128, 320)
    c[:, C_ID2:C_ID2 + 64] = (p == f).astype(np.float32)
    return c


def make_vecs(inp, b, half=0):
    v = np.zeros((128, NV), np.float32)
    v[:, V_NG:V_NG + 64] = fm(inp["norm_g"])
    v[:, V_FG:V_FG + 8] = fm(inp["final_g"])
    v[:, V_ADAB:V_ADAB + 192] = np.asarray(inp["ada_b"], np.float32).reshape(4, 48, 128).transpose(2, 0, 1).reshape(128, 192)
    v[:, V_CONVW:V_CONVW + 48] = fm(inp["conv_w"])
    v[:, V_MU:V_MU + 96] = fm(inp["rw_mu"])
    for name, col in (("rw_w0", V_W0), ("rw_a0", V_A0), ("rw_k_k", V_KK), ("rw_k_a", V_KA), ("rw_r_k", V_RK),
                      ("rw_ln_w", V_LNW), ("rw_ln_b", V_LNB)):
        v[:, col:col + 16] = fm(np.asarray(inp[name]).reshape(2, 1024))
    v[:, V_V0:V_V0 + 8] = fm(inp["rw_v0"])
    v[:, V_C:V_C + 8] = fm(np.asarray(inp["c"])[b])
    v[:, V_HM] = float(half)
    return v


_PROGS = {}


def get_prog(layers, fused=False):
    key = (tuple(layers), fused)
    if key not in _PROGS:
        _PROGS[key] = build(list(layers), fused=fused)
    return _PROGS[key]


def kernel(**inp):
    inp = {k: np.asarray(v) for k, v in inp.items()}
    x = inp["x"].astype(np.float32)
    ncore = 8
    consts = make_consts()
    layers = list(range(DEPTH))
    nc = get_prog(layers, fused=True)
    wl = weight_inputs(inp, layers)
    in_maps = []
    for c in range(ncore):
        b, half = c // 2, c % 2
        m = {"xT": np.ascontiguousarray(x[b, half * T:(half + 1) * T, :].T),
             "vecs": make_vecs(inp, b, half), "consts": consts}
        m.update(wl)
        in_maps.append(m)
    res = run_bass_kernel_spmd(nc, in_maps, core_ids=list(range(ncore))).results
    out = np.zeros((4, 2 * T, D), np.float32)
    for c in range(ncore):
        out[c // 2, (c % 2) * T:(c % 2 + 1) * T, :] = res[c]["outT"].T
    return out
```
